# Optimizing a Trainium2 kernel written in Bass

```python
import jax
import jax.numpy as jnp
from jax import lax
import numpy as np

D_MODEL = 2048
BATCH = 2
SEQ = 4096
DEPTH = 1

CTX_LEN = 256
GRID_W = 64
N_HEADS = 8
DH_QK = D_MODEL // 16
DH_V = D_MODEL // 8
D_QK = N_HEADS * DH_QK
D_V = N_HEADS * DH_V
D_CONV = D_MODEL // 2
CONV_W = 31
CONV_PAD = CONV_W // 2
D_FF = 256 * ((8 * D_MODEL // 3 + 255) // 256)
CHUNK = 128
N_GATE = 4 * N_HEADS
OFF_K = 0
OFF_V = OFF_K + D_QK
OFF_GATE = OFF_V + D_V
N_STATE_COLS = OFF_GATE + N_GATE
OFF_Q = N_STATE_COLS
OFF_O = OFF_Q + D_QK
OFF_GLU = OFF_O + D_V
OFF_MERGE = OFF_GLU + 2 * D_CONV
N_IN = OFF_MERGE + 2 * D_MODEL
DEEPNORM_ALPHA = (2.0 * DEPTH) ** 0.25
DEEPNORM_BETA = (8.0 * DEPTH) ** -0.25
FFN_RES_WEIGHT = 0.5
LN_EPS = 1e-5
M_INIT = -1e4

kernel_name = 'bidir_mlstm_conformer_macaron_hybrid'


def layer_norm(x, g, b=None):
    xf = x.astype(jnp.float32)
    mu = jnp.mean(xf, axis=-1, keepdims=True)
    var = jnp.mean(jnp.square(xf - mu), axis=-1, keepdims=True)
    y = (xf - mu) * lax.rsqrt(var + LN_EPS) * g
    if b is not None:
        y = y + b
    return y.astype(x.dtype)


def swiglu(h, w13, w2):
    a, g = jnp.split(h @ w13, 2, axis=-1)
    return (jax.nn.silu(a) * g) @ w2


def to_heads(p, dh):
    bsz, t, _ = p.shape
    return p.reshape(bsz, t, N_HEADS, dh).transpose(0, 2, 1, 3)


def flip_t(a):
    return jnp.flip(a, axis=2)


def zero_state(bsz):
    return (jnp.zeros((bsz, N_HEADS, DH_QK, DH_V), jnp.float32),
            jnp.zeros((bsz, N_HEADS, DH_QK), jnp.float32),
            jnp.full((bsz, N_HEADS), M_INIT, jnp.float32))


def mlstm_states(k, v, li, lf, state0):
    bsz, nh, t, dk = k.shape
    nc = t // CHUNK
    kc = k.reshape(bsz, nh, nc, CHUNK, dk)
    vc = v.reshape(bsz, nh, nc, CHUNK, v.shape[-1])
    lic = li.reshape(bsz, nh, nc, CHUNK)
    b = jnp.cumsum(lf.reshape(bsz, nh, nc, CHUNK), axis=-1)
    g = b[..., -1]
    a = g[..., None] - b + lic
    m_loc = jnp.max(a, axis=-1)
    kw = kc * jnp.exp(a - m_loc[..., None])[..., None]
    c_in = jnp.einsum('bhcld,bhcle->bhcde', kw, vc)
    n_in = jnp.sum(kw, axis=-2)

    def step(carry, inp):
        cm, nv, m = carry
        c_i, n_i, g_i, ml_i = inp
        m_new = jnp.maximum(g_i + m, ml_i)
        s_old = jnp.exp(g_i + m - m_new)
        s_in = jnp.exp(ml_i - m_new)
        c_new = s_old[..., None, None] * cm + s_in[..., None, None] * c_i
        n_new = s_old[..., None] * nv + s_in[..., None] * n_i
        return (c_new, n_new, m_new), (cm, nv, m)

    xs = (jnp.moveaxis(c_in, 2, 0), jnp.moveaxis(n_in, 2, 0),
          jnp.moveaxis(g, 2, 0), jnp.moveaxis(m_loc, 2, 0))
    final, starts = lax.scan(step, state0, xs)
    starts = (jnp.moveaxis(starts[0], 0, 2), jnp.moveaxis(starts[1], 0, 2), jnp.moveaxis(starts[2], 0, 2))
    return starts, final


def mlstm_outputs(q, k, v, li, lf, starts):
    c0, n0, m0 = starts
    bsz, nh, t, dk = q.shape
    nc = t // CHUNK
    qc = q.reshape(bsz, nh, nc, CHUNK, dk)
    kc = k.reshape(bsz, nh, nc, CHUNK, dk)
    vc = v.reshape(bsz, nh, nc, CHUNK, v.shape[-1])
    lic = li.reshape(bsz, nh, nc, CHUNK)
    b = jnp.cumsum(lf.reshape(bsz, nh, nc, CHUNK), axis=-1)
    lower = jnp.tril(jnp.ones((CHUNK, CHUNK), dtype=bool))
    d = jnp.where(lower, b[..., :, None] - b[..., None, :] + lic[..., None, :], -jnp.inf)
    m_inter = b + m0[..., None]
    m_j = jnp.maximum(m_inter, jnp.max(d, axis=-1))
    w_intra = jnp.exp(d - m_j[..., None])
    w_inter = jnp.exp(m_inter - m_j)
    s = jnp.einsum('bhcjd,bhcsd->bhcjs', qc, kc) * w_intra
    num = (jnp.einsum('bhcjs,bhcse->bhcje', s, vc)
           + w_inter[..., None] * jnp.einsum('bhcjd,bhcde->bhcje', qc, c0))
    den = jnp.sum(s, axis=-1) + w_inter * jnp.einsum('bhcjd,bhcd->bhcj', qc, n0)
    h = num / jnp.maximum(jnp.abs(den), jnp.exp(-m_j))[..., None]
    return h.reshape(bsz, nh, t, -1)


def mlstm_direction(q, k, v, li, lf, state0):
    starts, _ = mlstm_states(k, v, li, lf, state0)
    return mlstm_outputs(q, k, v, li, lf, starts)


def state_inputs(p):
    k = to_heads(p[..., OFF_K:OFF_V], DH_QK)
    v = to_heads(p[..., OFF_V:OFF_GATE], DH_V)
    gates = p[..., OFF_GATE:N_STATE_COLS].astype(jnp.float32)
    bsz, t, _ = gates.shape
    gates = gates.reshape(bsz, t, 4, N_HEADS).transpose(2, 0, 3, 1)
    fwd = (gates[0], jax.nn.log_sigmoid(gates[1]))
    bwd = (gates[2], jax.nn.log_sigmoid(gates[3]))
    return k, v, fwd, bwd


def depthwise_conv(u, w, pad_h, pad_w):
    return lax.conv_general_dilated(u, w.astype(u.dtype), (1, 1), (pad_h, pad_w),
                                    dimension_numbers=('NHWC', 'HWIO', 'NHWC'),
                                    feature_group_count=u.shape[-1])


def depthwise_conv_grid(u, w):
    bsz, t, ch = u.shape
    rows = t // GRID_W
    half = ch // 2
    grid = u.reshape(bsz, rows, GRID_W, ch)
    yh = depthwise_conv(grid[..., :half], w[:, :half].reshape(1, CONV_W, 1, half), (0, 0), (CONV_PAD, CONV_PAD))
    yv = depthwise_conv(grid[..., half:], w[:, half:].reshape(CONV_W, 1, 1, ch - half), (CONV_PAD, CONV_PAD), (0, 0))
    return jnp.concatenate([yh, yv], axis=-1).reshape(bsz, t, ch)


def depthwise_conv_tokens(u, w):
    ch = u.shape[-1]
    y = depthwise_conv(u[:, None], w.reshape(1, CONV_W, 1, ch), (0, 0), (CONV_PAD, CONV_PAD))
    return y[:, 0]


def hybrid_layer(x, ctx, c, c_ctx, w_ada, b_ada, w_in, b_in, mh_ln_g, conv_w, conv_b,
                 conv_ln_g, conv_ln_b, w_m_out, w_c_out, w_o, ffn1_w13, ffn1_w2,
                 ffn2_w13, ffn2_w2, post_ln_g, post_ln_b, ctx_out):
    bsz = x.shape[0]
    mod_x = (jax.nn.silu(c) @ w_ada + b_ada).reshape(bsz, 3, 3, D_MODEL)
    mod_c = (jax.nn.silu(c_ctx) @ w_ada + b_ada).reshape(1, 3, 3, D_MODEL)

    def mod_in(h, sub, mod):
        return h * (1.0 + mod[:, sub, 1][:, None]) + mod[:, sub, 0][:, None]

    def residual(h, sub, out, mod, weight):
        gate = mod[:, sub, 2][:, None]
        return layer_norm(DEEPNORM_ALPHA * h + weight * gate * out, post_ln_g[sub], post_ln_b[sub])

    def mlstm_finish(h, o):
        t = h.shape[2]
        h = layer_norm(h.transpose(0, 2, 1, 3), mh_ln_g.reshape(N_HEADS, DH_V))
        h = h.reshape(h.shape[0], t, D_V).astype(o.dtype) * jax.nn.sigmoid(o)
        return h @ w_m_out

    def conv_module(glu, grid):
        a, g = jnp.split(glu, 2, axis=-1)
        u = a * jax.nn.sigmoid(g)
        y = depthwise_conv_grid(u, conv_w) if grid else depthwise_conv_tokens(u, conv_w)
        y = layer_norm(y + conv_b, conv_ln_g, conv_ln_b)
        return jax.nn.silu(y) @ w_c_out

    def merge(p, y_m, y_c):
        g_m, g_c = jnp.split(p[..., OFF_MERGE:], 2, axis=-1)
        return (jax.nn.sigmoid(g_m) * y_m + jax.nn.sigmoid(g_c) * y_c) @ w_o

    x = residual(x, 0, swiglu(mod_in(x, 0, mod_x), ffn1_w13, ffn1_w2), mod_x, FFN_RES_WEIGHT)
    ctx = residual(ctx, 0, swiglu(mod_in(ctx, 0, mod_c), ffn1_w13, ffn1_w2), mod_c, FFN_RES_WEIGHT)

    hx = mod_in(x, 1, mod_x)
    hc = mod_in(ctx, 1, mod_c)
    px = hx @ w_in + b_in
    if ctx_out:
        pc = hc @ w_in + b_in
    else:
        pc = hc @ w_in[:, :N_STATE_COLS] + b_in[:N_STATE_COLS]

    kc, vc, (li_cf, lf_cf), (li_cb, lf_cb) = state_inputs(pc)
    zero = zero_state(bsz)
    st_cf, fin_cf = mlstm_states(kc, vc, li_cf, lf_cf, zero)
    st_cb, fin_cb = mlstm_states(flip_t(kc), flip_t(vc), flip_t(li_cb), flip_t(lf_cb), zero)

    kx, vx, (li_xf, lf_xf), (li_xb, lf_xb) = state_inputs(px)
    qx = to_heads(px[..., OFF_Q:OFF_O], DH_QK) * (DH_QK ** -0.5)
    h_x = (mlstm_direction(qx, kx, vx, li_xf, lf_xf, fin_cf)
           + flip_t(mlstm_direction(flip_t(qx), flip_t(kx), flip_t(vx), flip_t(li_xb), flip_t(lf_xb), fin_cb)))
    y_m = mlstm_finish(h_x, px[..., OFF_O:OFF_GLU])
    y_c = conv_module(px[..., OFF_GLU:OFF_MERGE], True)
    x = residual(x, 1, merge(px, y_m, y_c), mod_x, 1.0)

    if ctx_out:
        qc = to_heads(pc[..., OFF_Q:OFF_O], DH_QK) * (DH_QK ** -0.5)
        h_c = (mlstm_outputs(qc, kc, vc, li_cf, lf_cf, st_cf)
               + flip_t(mlstm_outputs(flip_t(qc), flip_t(kc), flip_t(vc), flip_t(li_cb), flip_t(lf_cb), st_cb)))
        yc_m = mlstm_finish(h_c, pc[..., OFF_O:OFF_GLU])
        yc_c = conv_module(pc[..., OFF_GLU:OFF_MERGE], False)
        ctx = residual(ctx, 1, merge(pc, yc_m, yc_c), mod_c, 1.0)

    x = residual(x, 2, swiglu(mod_in(x, 2, mod_x), ffn2_w13, ffn2_w2), mod_x, FFN_RES_WEIGHT)
    if ctx_out:
        ctx = residual(ctx, 2, swiglu(mod_in(ctx, 2, mod_c), ffn2_w13, ffn2_w2), mod_c, FFN_RES_WEIGHT)
        return x, ctx
    return x, None


def setup_inputs(seed: int = 0) -> dict:
    key = jax.random.key(seed)
    ks = jax.random.split(key, 24)

    def nrm(k, shape, scale):
        return jax.random.normal(k, shape, jnp.float32) * scale

    beta = DEEPNORM_BETA
    x = nrm(ks[0], (BATCH, SEQ, D_MODEL), 1.0)
    c = nrm(ks[1], (BATCH, D_MODEL), 1.0)
    ctx = nrm(ks[2], (BATCH, CTX_LEN, D_MODEL), 1.0)
    c_ctx = nrm(ks[3], (D_MODEL,), 1.0)
    w_ada = nrm(ks[4], (DEPTH, D_MODEL, 9 * D_MODEL), D_MODEL ** -0.5)
    b_ada = nrm(ks[5], (DEPTH, 9 * D_MODEL), 0.02)
    col_scale = jnp.concatenate([jnp.ones((OFF_V,), jnp.float32),
                                 jnp.full((D_V,), beta, jnp.float32),
                                 jnp.ones((N_IN - OFF_GATE,), jnp.float32)])
    w_in = nrm(ks[6], (DEPTH, D_MODEL, N_IN), D_MODEL ** -0.5) * col_scale
    f_bias = jnp.linspace(3.0, 6.0, N_HEADS, dtype=jnp.float32)
    gate_bias = jnp.concatenate([jnp.zeros((N_HEADS,), jnp.float32), f_bias,
                                 jnp.zeros((N_HEADS,), jnp.float32), f_bias])
    b_in = nrm(ks[7], (DEPTH, N_IN), 0.02).at[:, OFF_GATE:N_STATE_COLS].add(gate_bias)
    mh_ln_g = 1.0 + nrm(ks[8], (DEPTH, D_V), 0.02)
    conv_w = nrm(ks[9], (DEPTH, CONV_W, D_CONV), CONV_W ** -0.5)
    conv_b = nrm(ks[10], (DEPTH, D_CONV), 0.02)
    conv_ln_g = 1.0 + nrm(ks[11], (DEPTH, D_CONV), 0.02)
    conv_ln_b = nrm(ks[12], (DEPTH, D_CONV), 0.02)
    w_m_out = nrm(ks[13], (DEPTH, D_V, D_MODEL), D_V ** -0.5 * beta)
    w_c_out = nrm(ks[14], (DEPTH, D_CONV, D_MODEL), D_CONV ** -0.5 * beta)
    w_o = nrm(ks[15], (DEPTH, D_MODEL, D_MODEL), D_MODEL ** -0.5 * beta)
    ffn1_w13 = nrm(ks[16], (DEPTH, D_MODEL, 2 * D_FF), D_MODEL ** -0.5 * beta)
    ffn1_w2 = nrm(ks[17], (DEPTH, D_FF, D_MODEL), D_FF ** -0.5 * beta)
    ffn2_w13 = nrm(ks[18], (DEPTH, D_MODEL, 2 * D_FF), D_MODEL ** -0.5 * beta)
    ffn2_w2 = nrm(ks[19], (DEPTH, D_FF, D_MODEL), D_FF ** -0.5 * beta)
    post_ln_g = 1.0 + nrm(ks[20], (DEPTH, 3, D_MODEL), 0.02)
    post_ln_b = nrm(ks[21], (DEPTH, 3, D_MODEL), 0.02)
    return {'x': x, 'c': c, 'ctx': ctx, 'c_ctx': c_ctx, 'w_ada': w_ada, 'b_ada': b_ada,
            'w_in': w_in, 'b_in': b_in, 'mh_ln_g': mh_ln_g, 'conv_w': conv_w, 'conv_b': conv_b,
            'conv_ln_g': conv_ln_g, 'conv_ln_b': conv_ln_b, 'w_m_out': w_m_out, 'w_c_out': w_c_out,
            'w_o': w_o, 'ffn1_w13': ffn1_w13, 'ffn1_w2': ffn1_w2, 'ffn2_w13': ffn2_w13,
            'ffn2_w2': ffn2_w2, 'post_ln_g': post_ln_g, 'post_ln_b': post_ln_b}


def reference(x, c, ctx, c_ctx, w_ada, b_ada, w_in, b_in, mh_ln_g, conv_w, conv_b, conv_ln_g,
              conv_ln_b, w_m_out, w_c_out, w_o, ffn1_w13, ffn1_w2, ffn2_w13, ffn2_w2,
              post_ln_g, post_ln_b):
    for l in range(DEPTH):
        x, ctx = hybrid_layer(x, ctx, c, c_ctx, w_ada[l], b_ada[l], w_in[l], b_in[l], mh_ln_g[l],
                              conv_w[l], conv_b[l], conv_ln_g[l], conv_ln_b[l], w_m_out[l],
                              w_c_out[l], w_o[l], ffn1_w13[l], ffn1_w2[l], ffn2_w13[l], ffn2_w2[l],
                              post_ln_g[l], post_ln_b[l], l < DEPTH - 1)
    return x
```

```python
import numpy as np
from contextlib import ExitStack
import concourse.bass as bass
import concourse.mybir as mybir
from concourse.bass_utils import run_bass_kernel_spmd

F32 = mybir.dt.float32
BF16 = mybir.dt.bfloat16
AF = mybir.ActivationFunctionType
ALU = mybir.AluOpType
AX = mybir.AxisListType

NCORES = 8
D = 2048
KT = 16
NT = 1024
NCTX = 256
TOK = NT + NCTX
DFF = 5632
NIN = 12320
OFF_V = 1024
OFF_GATE = 3072
OFF_Q = 3104
OFF_O = 4128
OFF_GLU = 6176
OFF_MERGE = 8224
ALPHA = 2.0 ** 0.25
EPS = 1e-5
NEG = -30000.0
XW = 2 * 8 * 257 + 32 + 4 * 1024
XOFF_SC = 2 * 8 * 257
XOFF_U = XOFF_SC + 32
NFLAG = 48


class Sem:
    def __init__(self, b, name):
        self.h = b.es.enter_context(b.nc.semaphore(name))
        self.name = name


def _add(d, t):
    if t is not None:
        if d.get(t[0], 0) < t[1]:
            d[t[0]] = t[1]


class Buf:
    __slots__ = ("w", "r")

    def __init__(self, fence=None):
        self.w = None
        self.r = dict(fence) if fence else {}


class Eng:
    def __init__(self, b, name, h, is_pe=False):
        self.b = b
        self.name = name
        self.h = h
        self.is_pe = is_pe
        self.sem = Sem(b, "s_" + name)
        self.count = 0
        self.seen = {}

    def waits(self, deps):
        for sem, v in deps.items():
            if self.is_pe and sem is self.sem:
                continue
            if sem is self.sem and v <= self.count - 3:
                continue
            if self.seen.get(sem, 0) >= v:
                continue
            self.seen[sem] = v
            self.h.wait_ge(sem.h, v)

    def issue(self, fn, deps, inc=True):
        self.waits(deps)
        ins = fn(self.h)
        if inc:
            self.count += 1
            ins.then_inc(self.sem.h, 1)
            return (self.sem, self.count)
        return (self.sem, self.count + 1)


class DmaQ:
    def __init__(self, b, eng, n, nslot=0):
        self.b = b
        self.eng = eng
        self.nfree = n
        self.sems = [Sem(b, "d_%s%d" % (eng.name, i)) for i in range(n + nslot)]
        self.cnt = [0] * (n + nslot)
        self.nxt = 0

    def dma(self, out, in_, reads=(), writes=(), slot=None):
        if slot is None:
            i = self.nxt % self.nfree
        else:
            i = self.nfree + slot
        self.nxt += 1
        sem = self.sems[i]
        deps = self.b.deps_for(reads, writes)
        if self.cnt[i] and slot is None:
            _add(deps, (sem, 16 * self.cnt[i]))
        self.eng.waits(deps)
        self.eng.h.dma_start(out=out, in_=in_).then_inc(sem.h, 16)
        self.cnt[i] += 1
        tk = (sem, 16 * self.cnt[i])
        self.b.note(reads, writes, tk)
        return tk

    def outstanding(self):
        return {s: 16 * c for s, c in zip(self.sems, self.cnt) if c}


class Scope:
    def __init__(self, b):
        self.b = b
        self.es = ExitStack()
        self.bufs = []

    def __enter__(self):
        self.es.__enter__()
        return self

    def __exit__(self, *a):
        for bf in self.bufs:
            _add(self.b.fence, bf.w)
            for s, v in bf.r.items():
                _add(self.b.fence, (s, v))
        return self.es.__exit__(*a)

    def sb(self, name, shape, dt):
        self.b.uid += 1
        return self.es.enter_context(self.b.nc.sbuf_tensor("%s_%d" % (name, self.b.uid), list(shape), dt))

    def buf(self):
        bf = Buf(self.b.fence)
        self.bufs.append(bf)
        return bf

    def bufs_n(self, *dims):
        if len(dims) == 1:
            return [self.buf() for _ in range(dims[0])]
        return [self.bufs_n(*dims[1:]) for _ in range(dims[0])]


class Ring:
    def __init__(self, sc, name, n, shape, dt):
        self.t = [sc.sb("%s%d" % (name, i), shape, dt) for i in range(n)]
        self.bf = [sc.buf() for _ in range(n)]
        self.i = 0

    def next(self):
        j = self.i % len(self.t)
        self.i += 1
        return self.t[j], self.bf[j]


class Piece:
    def __init__(self, s, n, ctx):
        self.s = s
        self.n = n
        self.ctx = ctx
        self.r = 1 if ctx else 0


class Bld:
    def __init__(self):
        self.nc = bass.Bass("TRN2", target_bir_lowering=False)
        self.es = ExitStack()
        nc = self.nc
        self.pe = Eng(self, "pe", nc.tensor, True)
        self.act = Eng(self, "act", nc.scalar)
        self.dve = Eng(self, "dve", nc.vector)
        self.pool = Eng(self, "pool", nc.gpsimd)
        self.sp = Eng(self, "sp", nc.sync)
        self.qsp = DmaQ(self, self.sp, 8)
        self.qpl = DmaQ(self, self.pool, 4, nslot=4)
        self.fence = {}
        self.in_names = []
        self.dram = {}
        self.banks = []
        self.bank_i = 0
        self.uid = 0

    def deps_for(self, reads, writes):
        d = {}
        for bf in reads:
            _add(d, bf.w)
        for bf in writes:
            _add(d, bf.w)
            for s, v in bf.r.items():
                _add(d, (s, v))
        return d

    def note(self, reads, writes, tk):
        for bf in reads:
            if bf.r.get(tk[0], 0) < tk[1]:
                bf.r[tk[0]] = tk[1]
        for bf in writes:
            bf.w = tk
            bf.r = {}

    def op(self, eng, fn, reads=(), writes=(), inc=True):
        deps = self.deps_for(reads, writes)
        tk = eng.issue(fn, deps, inc)
        self.note(reads, writes, tk)
        return tk

    def din(self, name, shape, dt=F32):
        t = self.nc.dram_tensor(name, list(shape), dt, kind="ExternalInput")
        self.in_names.append(name)
        self.dram[name] = t
        return t

    def dscratch(self, name, shape, dt, **kw):
        t = self.nc.dram_tensor(name, list(shape), dt, **kw)
        return t

    def prewait(self, eng, bank_pairs):
        eng.waits(self.deps_for([], [bb for (_, bb) in bank_pairs]))

    def next_bank(self):
        j = self.bank_i % len(self.banks)
        self.bank_i += 1
        return self.banks[j]


def build(stop_after=2):
    b = Bld()
    nc = b.nc
    pe, act, dve = b.pe, b.act, b.dve
    main = Scope(b)
    main.__enter__()

    x_d = b.din("x", [NT, D])
    ctx_d = b.din("ctx", [NCTX, D])
    cvec_d = b.din("cvec", [2, D])
    flags_d = b.din("flags", [128, NFLAG])
    consts_d = b.din("consts", [128, 512])
    w_ada_d = b.din("w_ada", [D, 9 * D])
    b_ada_d = b.din("b_ada", [1, 9 * D])
    f1w13_d = b.din("ffn1_w13", [D, 2 * DFF])
    f1w2_d = b.din("ffn1_w2", [DFF, D])
    pg_d = b.din("post_ln_g", [3, D])
    pb_d = b.din("post_ln_b", [3, D])
    if stop_after >= 1:
        w_in_d = b.din("w_in", [D, NIN])
        b_in_d = b.din("b_in", [1, NIN])
        mhg_d = b.din("mh_ln_g", [1, D])
        cw_d = b.din("conv_w", [31, 1024])
        cb_d = b.din("conv_b", [1, 1024])
        clg_d = b.din("conv_ln_g", [1, 1024])
        clb_d = b.din("conv_ln_b", [1, 1024])
        wmo_d = b.din("w_m_out", [D, D])
        wco_d = b.din("w_c_out", [1024, D])
        wo_d = b.din("w_o", [D, D])
    if stop_after >= 2:
        f2w13_d = b.din("ffn2_w13", [D, 2 * DFF])
        f2w2_d = b.din("ffn2_w2", [DFF, D])
    y_d = nc.dram_tensor("y", [NT, D], F32, kind="ExternalOutput")

    XL = main.sb("XL", [128, KT, NT], F32)
    HL = main.sb("HL", [128, KT, NT], BF16)
    XLb = main.bufs_n(KT, 2)
    HLb = main.bufs_n(KT, 2)
    NPAN = 3
    PAN = Ring(main, "pan", NPAN, [128, 16, 256], BF16)
    CONST = main.sb("CONST", [128, 512], F32)
    CONSTb = main.buf()
    IDENT = CONST[:, 0:128]
    ULE = CONST[:, 128:256]
    UGE = CONST[:, 256:384]
    ONES = CONST[:, 384:512]
    NPRM = 648
    PRM = main.sb("PRM", [128, NPRM], F32)
    PRMb = main.buf()
    MOD = main.sb("MOD", [128, 9, 16, 2], F32)
    MODb = [main.buf() for _ in range(9)]
    DER = main.sb("DER", [128, 8, 16, 2], F32)
    DERb = main.buf()
    SC = main.sb("SC", [128, 16, 2], BF16)
    SCb = main.buf()
    for i in range(8):
        t = main.es.enter_context(nc.psum_tensor("bank%d" % i, [128, 512], F32))
        b.banks.append((t, main.buf()))

    b.qsp.dma(CONST[:, :], consts_d[:, :], writes=[CONSTb])

    if stop_after >= 1:
        GATES = main.sb("GATES", [128, 10, 32], F32)
        GATESb = main.buf()
        GD = {}
        for nm in ("LF", "GB", "BB", "RR", "RMB", "WW", "CL", "SFX", "LAM", "WA"):
            GD[nm] = main.sb("gd_" + nm, [128, 2, 10, 8], F32)
        GDb = main.buf()
        RM2 = main.sb("RM2", [128, 2], F32)
        DG = main.sb("DG", [128, 2, 80], F32)
        MAGG = main.sb("MAGG", [128, 2, 8], F32)
        MCTX = main.sb("MCTX", [128, 2, 8], F32)
        SCAL = main.sb("SCAL", [128, 32], F32)
        FL = main.sb("FL", [128, NFLAG], F32)
        FLb = main.buf()
        SCS = main.sb("SCS", [128, 8, 32], F32)
        CMB = {}
        for nm in ("GF", "SG", "LJ"):
            CMB[nm] = main.sb("cm_" + nm, [128, 2, 8, 8], F32)
        CO = main.sb("CO", [128, 2, 9, 8], F32)
        MST = main.sb("MST", [128, 2, 8], F32)
        CMb = main.buf()
        MS = main.sb("MS", [128, 2, 9, 8], F32)
        MXS = main.sb("MXS", [128, 2, 8, 8], F32)
        EXA = main.sb("EXA", [128, 3, 2, 8, 8], F32)
        P2b = main.buf()
        b.qsp.dma(FL[:, :], flags_d[:, :], writes=[FLb])

    srcs = [("bada", b_ada_d[0:1, :].rearrange("o (r p) -> (o r) p", p=128), 144),
            ("pg", pg_d[:, :].rearrange("s (t p) -> (s t) p", p=128), 48),
            ("pb", pb_d[:, :].rearrange("s (t p) -> (s t) p", p=128), 48),
            ("cvec", cvec_d[:, :].rearrange("s (t p) -> (s t) p", p=128), 32)]
    if stop_after >= 1:
        def bi(o, n):
            return b_in_d[0:1, o:o + n].rearrange("o (r p) -> (o r) p", p=128)
        srcs += [("mhg", mhg_d[0:1, :].rearrange("o (r p) -> (o r) p", p=128), 16),
                 ("cb", cb_d[0:1, :].rearrange("o (r p) -> (o r) p", p=128), 8),
                 ("clg", clg_d[0:1, :].rearrange("o (r p) -> (o r) p", p=128), 8),
                 ("clb", clb_d[0:1, :].rearrange("o (r p) -> (o r) p", p=128), 8),
                 ("cw", cw_d[:, :].rearrange("j (t p) -> (j t) p", p=128), 248),
                 ("bk", bi(0, 1024), 8), ("bq", bi(OFF_Q, 1024), 8), ("bo", bi(OFF_O, 2048), 16),
                 ("bglu", bi(OFF_GLU, 2048), 16), ("bmg", bi(OFF_MERGE, 4096), 32)]
    pbase = {}
    row = 0
    with Scope(b) as sc:
        STG = sc.sb("STG", [128, 6, 128], F32)
        STGb = [sc.buf() for _ in range(6)]
        for name, ap, n in srcs:
            pbase[name] = row
            o = 0
            while o < n:
                c, r0 = divmod(row + o, 128)
                m = min(n - o, 128 - r0)
                b.qsp.dma(STG[r0:r0 + m, c, :], ap[o:o + m, :], writes=[STGb[c]])
                o += m
            row += n
        assert row <= NPRM
        nch = (row + 127) // 128
        for c in range(nch):
            nr = min(128, row - c * 128)
            bk, bkb = b.next_bank()
            b.op(pe, lambda e: e.transpose(out=bk[:, 0:nr], in_=STG[0:nr, c, :], identity=CONST[0:nr, 0:nr]),
                 reads=[STGb[c], CONSTb], writes=[bkb])
            b.op(dve, lambda e: e.tensor_copy(out=PRM[:, c * 128:c * 128 + nr], in_=bk[:, 0:nr]),
                 reads=[bkb], writes=[PRMb])

    def P(name, i=0, n=1):
        o = pbase[name] + i
        return PRM[:, o:o + n]

    for r in range(2):
        b.op(act, lambda e: e.activation(out=SC[:, :, r], in_=P("cvec", r * 16, 16), func=AF.Silu),
             reads=[PRMb], writes=[SCb])

    def load_panel(wd, r0, nk, c0, ncols):
        slot = PAN.i % len(PAN.t)
        pt, pb_ = PAN.next()
        src = wd[r0:r0 + nk * 128, c0:c0 + ncols].rearrange("(kt p) n -> p kt n", p=128)
        b.qpl.dma(pt[:, 0:nk, 0:ncols], src, writes=[pb_], slot=slot)
        return pt, pb_

    def mod_vec(v):
        bk, bkb = b.next_bank()
        for j in range(8):
            pt, pb_ = load_panel(w_ada_d, 0, 16, v * 2048 + j * 256, 256)
            for ct in range(2):
                t = 2 * j + ct
                for kt in range(16):
                    b.op(pe, lambda e: e.matmul(bk[:, 2 * t:2 * t + 2], lhsT=pt[:, kt, ct * 128:(ct + 1) * 128],
                                               rhs=SC[:, kt, :], start=(kt == 0), stop=(kt == 15)),
                         reads=[pb_, SCb], writes=[bkb], inc=(kt == 15))
        src = bk[:, 0:32].rearrange("p (t r) -> p t r", r=2)
        for r in range(2):
            b.op(dve, lambda e: e.tensor_tensor(out=MOD[:, v, :, r], in0=src[:, :, r], in1=P("bada", v * 16, 16),
                                               op=ALU.add), reads=[bkb, PRMb], writes=[MODb[v]])

    P0 = Piece(0, 512, False)
    P1 = Piece(512, 512, False)

    def xap(XC, kt, p):
        if p.ctx:
            return XC[:, kt, 0:p.n]
        return XL[:, kt, p.s:p.s + p.n]

    def hap(HC, kt, p):
        if p.ctx:
            return HC[:, kt, 0:p.n]
        return HL[:, kt, p.s:p.s + p.n]

    def derive(sub_ln, next_sub, final):
        g = P("pg", sub_ln * 16, 16)
        bb = P("pb", sub_ln * 16, 16)
        a = 1.0 if final else ALPHA
        for r in range(2):
            b.op(dve, lambda e: e.tensor_scalar(out=DER[:, 0, :, r], in0=g, scalar1=a, scalar2=None, op0=ALU.mult),
                 reads=[PRMb], writes=[DERb])
            b.op(dve, lambda e: e.tensor_scalar(out=DER[:, 1, :, r], in0=bb, scalar1=a, scalar2=None, op0=ALU.mult),
                 reads=[PRMb], writes=[DERb])
            if next_sub is not None:
                vs, vh = 3 * next_sub, 3 * next_sub + 1
                b.op(dve, lambda e: e.tensor_scalar(out=DER[:, 4, :, r], in0=MOD[:, vh, :, r], scalar1=1.0,
                                                   scalar2=None, op0=ALU.add),
                     reads=[MODb[vh]], writes=[DERb])
                b.op(dve, lambda e: e.tensor_tensor(out=DER[:, 2, :, r], in0=DER[:, 4, :, r], in1=g, op=ALU.mult),
                     reads=[DERb, PRMb], writes=[DERb])
                b.op(dve, lambda e: e.tensor_tensor(out=DER[:, 3, :, r], in0=DER[:, 4, :, r], in1=bb, op=ALU.mult),
                     reads=[DERb, PRMb], writes=[DERb])
                b.op(dve, lambda e: e.tensor_tensor(out=DER[:, 3, :, r], in0=DER[:, 3, :, r], in1=MOD[:, vs, :, r],
                                                   op=ALU.add),
                     reads=[DERb, MODb[vs]], writes=[DERb])

    def gate_scalars(sub, weight):
        vg = 3 * sub + 2
        for r in range(2):
            b.op(dve, lambda e: e.tensor_scalar(out=DER[:, 5, :, r], in0=MOD[:, vg, :, r], scalar1=weight,
                                               scalar2=None, op0=ALU.mult),
                 reads=[MODb[vg]], writes=[DERb])

    def layer_norm_x(sc, pieces, XC, XCb, HC, HCb, do_h, write_x_ctx=False):
        LNT = Ring(sc, "lnt", 3, [128, 512], F32)
        MEAN = sc.sb("ln_mean", [128, 512], F32)
        RSTD = sc.sb("ln_rstd", [128, 512], F32)
        MSQ = sc.sb("ln_msq", [128, 512], F32)
        stb = sc.buf()
        for pi, p in enumerate(pieces):
            n = p.n
            xb = (lambda kt: XCb[kt]) if p.ctx else (lambda kt: XLb[kt][pi])
            hb = (lambda kt: HCb[kt]) if p.ctx else (lambda kt: HLb[kt][pi])
            s1, s1b = b.next_bank()
            for kt in range(KT):
                b.op(pe, lambda e: e.matmul(s1[:, 0:n], lhsT=ONES, rhs=xap(XC, kt, p), start=(kt == 0),
                                           stop=(kt == KT - 1)),
                     reads=[xb(kt), CONSTb], writes=[s1b], inc=(kt == KT - 1))
            s2, s2b = b.next_bank()
            for kt in range(KT):
                sq, sqb = LNT.next()
                b.op(act, lambda e: e.activation(out=sq[:, 0:n], in_=xap(XC, kt, p), func=AF.Square),
                     reads=[xb(kt)], writes=[sqb])
                b.op(pe, lambda e: e.matmul(s2[:, 0:n], lhsT=ONES, rhs=sq[:, 0:n], start=(kt == 0),
                                           stop=(kt == KT - 1)),
                     reads=[sqb, CONSTb], writes=[s2b], inc=True)
            b.op(dve, lambda e: e.tensor_scalar(out=MEAN[:, 0:n], in0=s1[:, 0:n], scalar1=1.0 / D, scalar2=None,
                                               op0=ALU.mult), reads=[s1b], writes=[stb])
            b.op(dve, lambda e: e.tensor_tensor(out=MSQ[:, 0:n], in0=MEAN[:, 0:n], in1=MEAN[:, 0:n], op=ALU.mult),
                 reads=[stb], writes=[stb])
            b.op(dve, lambda e: e.scalar_tensor_tensor(out=MSQ[:, 0:n], in0=s2[:, 0:n], scalar=1.0 / D,
                                                      in1=MSQ[:, 0:n], op0=ALU.mult, op1=ALU.subtract),
                 reads=[s2b, stb], writes=[stb])
            b.op(dve, lambda e: e.tensor_scalar(out=MSQ[:, 0:n], in0=MSQ[:, 0:n], scalar1=EPS, scalar2=None,
                                               op0=ALU.add), reads=[stb], writes=[stb])
            b.op(act, lambda e: e.activation(out=RSTD[:, 0:n], in_=MSQ[:, 0:n], func=AF.Sqrt),
                 reads=[stb], writes=[stb])
            b.op(dve, lambda e: e.reciprocal(out=RSTD[:, 0:n], in_=RSTD[:, 0:n]), reads=[stb], writes=[stb])
            r = p.r
            for kt in range(KT):
                tt, ttb = LNT.next()
                b.op(dve, lambda e: e.tensor_tensor(out=tt[:, 0:n], in0=xap(XC, kt, p), in1=MEAN[:, 0:n],
                                                   op=ALU.subtract), reads=[xb(kt), stb], writes=[ttb])
                b.op(dve, lambda e: e.tensor_tensor(out=tt[:, 0:n], in0=tt[:, 0:n], in1=RSTD[:, 0:n], op=ALU.mult),
                     reads=[ttb, stb], writes=[ttb])
                if (not p.ctx) or write_x_ctx:
                    b.op(act, lambda e: e.activation(out=xap(XC, kt, p), in_=tt[:, 0:n], func=AF.Identity,
                                                    scale=DER[:, 0, kt, r:r + 1], bias=DER[:, 1, kt, r:r + 1]),
                         reads=[ttb, DERb], writes=[xb(kt)])
                if do_h:
                    b.op(act, lambda e: e.activation(out=hap(HC, kt, p), in_=tt[:, 0:n], func=AF.Identity,
                                                    scale=DER[:, 2, kt, r:r + 1], bias=DER[:, 3, kt, r:r + 1]),
                         reads=[ttb, DERb], writes=[hb(kt)])

    FSPLITS = [(0, 12), (12, 12), (24, 12), (36, 8)]

    def ffn(sc, w13_d, w2_d, pieces, XC, XCb, HC, HCb, between=None):
        ntok = sum(p.n for p in pieces)
        UT = sc.sb("UT", [128, 12, ntok], BF16)
        UTb = sc.bufs_n(12, len(pieces))
        TMP = Ring(sc, "ffn_tmp", 3, [128, 512], F32)
        offs = []
        o = 0
        for p in pieces:
            offs.append(o)
            o += p.n
        for si, (f0, nf) in enumerate(FSPLITS):
            for fg in range(0, nf, 2):
                pa, pab = load_panel(w13_d, 0, 16, (f0 + fg) * 128, 256)
                pg_, pgb = load_panel(w13_d, 0, 16, DFF + (f0 + fg) * 128, 256)
                for ct in range(2):
                    fl = fg + ct
                    res = []
                    allb = [b.next_bank() for _ in range(2 * len(pieces))]
                    b.prewait(pe, allb)
                    for gi, (pt, pb_) in enumerate(((pa, pab), (pg_, pgb))):
                        bks = allb[gi * len(pieces):(gi + 1) * len(pieces)]
                        for kt in range(KT):
                            for pi, p in enumerate(pieces):
                                hb = HCb[kt] if p.ctx else HLb[kt][pi]
                                b.op(pe, lambda e: e.matmul(bks[pi][0][:, 0:p.n],
                                                           lhsT=pt[:, kt, ct * 128:(ct + 1) * 128],
                                                           rhs=hap(HC, kt, p), start=(kt == 0), stop=(kt == KT - 1)),
                                     reads=[pb_, hb], writes=[bks[pi][1]],
                                     inc=(kt == KT - 1 and pi == len(pieces) - 1))
                        res.append(bks)
                    for pi, p in enumerate(pieces):
                        tm, tmb = TMP.next()
                        ab, abb = res[0][pi]
                        gb, gbb = res[1][pi]
                        b.op(act, lambda e: e.activation(out=tm[:, 0:p.n], in_=ab[:, 0:p.n], func=AF.Silu),
                             reads=[abb], writes=[tmb])
                        b.op(dve, lambda e: e.tensor_tensor(out=UT[:, fl, offs[pi]:offs[pi] + p.n], in0=tm[:, 0:p.n],
                                                           in1=gb[:, 0:p.n], op=ALU.mult),
                             reads=[tmb, gbb], writes=[UTb[fl][pi]])
            if between is not None:
                between(si, 0)
            for dg in range(0, KT, 2):
                pw, pwb = load_panel(w2_d, f0 * 128, nf, dg * 128, 256)
                for ct in range(2):
                    dt_ = dg + ct
                    bks = [b.next_bank() for _ in pieces]
                    b.prewait(pe, bks)
                    for kf in range(nf):
                        for pi, p in enumerate(pieces):
                            b.op(pe, lambda e: e.matmul(bks[pi][0][:, 0:p.n], lhsT=pw[:, kf, ct * 128:(ct + 1) * 128],
                                                       rhs=UT[:, kf, offs[pi]:offs[pi] + p.n], start=(kf == 0),
                                                       stop=(kf == nf - 1)),
                                 reads=[pwb, UTb[kf][pi]], writes=[bks[pi][1]],
                                 inc=(kf == nf - 1 and pi == len(pieces) - 1))
                    for pi, p in enumerate(pieces):
                        xb = XCb[dt_] if p.ctx else XLb[dt_][pi]
                        yb, ybb = bks[pi]
                        b.op(dve, lambda e: e.scalar_tensor_tensor(out=xap(XC, dt_, p), in0=yb[:, 0:p.n],
                                                                  scalar=DER[:, 5, dt_, p.r:p.r + 1],
                                                                  in1=xap(XC, dt_, p), op0=ALU.mult, op1=ALU.add),
                             reads=[ybb, DERb, xb], writes=[xb])
            if between is not None:
                between(si, 1)

    def mod_in_and_prescale(pieces, XC, XCb, HC, HCb, sub):
        vs, vh = 3 * sub, 3 * sub + 1
        for r in range(2):
            b.op(dve, lambda e: e.tensor_scalar(out=DER[:, 4, :, r], in0=MOD[:, vh, :, r], scalar1=1.0, scalar2=None,
                                               op0=ALU.add), reads=[MODb[vh]], writes=[DERb])
        for kt in range(KT):
            for pi, p in enumerate(pieces):
                xb = XCb[kt] if p.ctx else XLb[kt][pi]
                hb = HCb[kt] if p.ctx else HLb[kt][pi]
                b.op(act, lambda e: e.activation(out=hap(HC, kt, p), in_=xap(XC, kt, p), func=AF.Identity,
                                                scale=DER[:, 4, kt, p.r:p.r + 1], bias=MOD[:, vs, kt, p.r:p.r + 1]),
                     reads=[xb, DERb, MODb[vs]], writes=[hb])
                b.op(dve, lambda e: e.tensor_scalar(out=xap(XC, kt, p), in0=xap(XC, kt, p), scalar1=ALPHA,
                                                   scalar2=None, op0=ALU.mult), reads=[xb], writes=[xb])

    s1 = Scope(b)
    s1.__enter__()
    XC = s1.sb("XC", [128, KT, NCTX], F32)
    HC = s1.sb("HC", [128, KT, NCTX], BF16)
    XCb = s1.bufs_n(KT)
    HCb = s1.bufs_n(KT)
    P2c = Piece(0, NCTX, True)
    pieces0 = [P0, P1, P2c]

    with Scope(b) as sc:
        XST = Ring(sc, "xst", 2, [128, D], F32)
        for tt in range(10):
            st, stb_ = XST.next()
            if tt < 8:
                b.qsp.dma(st[:, :], x_d[tt * 128:(tt + 1) * 128, :], writes=[stb_])
            else:
                b.qsp.dma(st[:, :], ctx_d[(tt - 8) * 128:(tt - 7) * 128, :], writes=[stb_])
            for g in range(4):
                bk, bkb = b.next_bank()
                for j in range(4):
                    kt = 4 * g + j
                    b.op(pe, lambda e: e.transpose(out=bk[:, j * 128:(j + 1) * 128],
                                                  in_=st[:, kt * 128:(kt + 1) * 128], identity=IDENT),
                         reads=[stb_, CONSTb], writes=[bkb], inc=(j == 3))
                src = bk[:, :].rearrange("p (j t) -> p j t", t=128)
                if tt < 8:
                    dst = XL[:, 4 * g:4 * g + 4, tt * 128:(tt + 1) * 128]
                    wb = [XLb[4 * g + j][tt // 4] for j in range(4)]
                else:
                    dst = XC[:, 4 * g:4 * g + 4, (tt - 8) * 128:(tt - 7) * 128]
                    wb = [XCb[4 * g + j] for j in range(4)]
                eng = act if (g % 2 == 0) else dve
                if eng is act:
                    b.op(act, lambda e: e.activation(out=dst, in_=src, func=AF.Identity), reads=[bkb], writes=wb)
                else:
                    b.op(dve, lambda e: e.tensor_copy(out=dst, in_=src), reads=[bkb], writes=wb)

    mod_vec(0)
    mod_vec(1)
    mod_in_and_prescale(pieces0, XC, XCb, HC, HCb, 0)

    def between0(si, half):
        if si == 0 and half == 0:
            mod_vec(2)
            gate_scalars(0, 0.5)
        elif si == 1 and half == 1:
            mod_vec(3)
            mod_vec(4)
        elif si == 2 and half == 1:
            mod_vec(5)
        elif si == 3 and half == 0:
            mod_vec(6)
            mod_vec(7)

    with Scope(b) as sc:
        ffn(sc, f1w13_d, f1w2_d, pieces0, XC, XCb, HC, HCb, between=between0)
    derive(0, 1, final=(stop_after == 0))
    with Scope(b) as sc:
        layer_norm_x(sc, pieces0, XC, XCb, HC, HCb, do_h=True, write_x_ctx=(stop_after == 0))

    if stop_after == 0:
        ydbg = nc.dram_tensor("yctx", [NCTX, D], F32, kind="ExternalOutput")
    s1_open = True

    def write_out():
        with Scope(b) as sc:
            OST = Ring(sc, "ost", 2, [128, D], F32)
            ntile = 10 if stop_after == 0 else 8
            for tt in range(ntile):
                ot, otb = OST.next()
                for g in range(4):
                    bk, bkb = b.next_bank()
                    for j in range(4):
                        kt = 4 * g + j
                        if tt < 8:
                            src = XL[:, kt, tt * 128:(tt + 1) * 128]
                            rb = XLb[kt][tt // 4]
                        else:
                            src = XC[:, kt, (tt - 8) * 128:(tt - 7) * 128]
                            rb = XCb[kt]
                        b.op(pe, lambda e: e.transpose(out=bk[:, j * 128:(j + 1) * 128], in_=src, identity=IDENT),
                             reads=[rb, CONSTb], writes=[bkb], inc=(j == 3))
                    if g % 2 == 0:
                        b.op(act, lambda e: e.activation(out=ot[:, g * 512:(g + 1) * 512], in_=bk[:, :], func=AF.Identity),
                             reads=[bkb], writes=[otb])
                    else:
                        b.op(dve, lambda e: e.tensor_copy(out=ot[:, g * 512:(g + 1) * 512], in_=bk[:, :]),
                             reads=[bkb], writes=[otb])
                if tt < 8:
                    b.qsp.dma(y_d[tt * 128:(tt + 1) * 128, :], ot[:, :], reads=[otb])
                else:
                    b.qsp.dma(ydbg[(tt - 8) * 128:(tt - 7) * 128, :], ot[:, :], reads=[otb])
        fin = {}
        fin.update(b.qsp.outstanding())
        b.sp.waits(fin)

    if stop_after == 0:
        write_out()
        s1.__exit__(None, None, None)
        main.__exit__(None, None, None)
        return b

    if stop_after == 3:
        s1.__exit__(None, None, None)
        piecesL = [P0, P1]
        mod_vec(8)
        gate_scalars(2, 0.5)
        with Scope(b) as sc:
            ffn(sc, f2w13_d, f2w2_d, piecesL, None, None, None, None)
        derive(2, None, final=True)
        with Scope(b) as sc:
            layer_norm_x(sc, piecesL, None, None, None, None, do_h=False)
        write_out()
        main.__exit__(None, None, None)
        return b

    def tt_(out, in0, in1, op, reads, writes, eng=None):
        return b.op(eng or dve, lambda e: e.tensor_tensor(out=out, in0=in0, in1=in1, op=op), reads, writes)

    def ts_(out, in0, s1_, op0, reads, writes, s2_=None, op1=None):
        if op1 is None:
            return b.op(dve, lambda e: e.tensor_scalar(out=out, in0=in0, scalar1=s1_, scalar2=None, op0=op0),
                        reads, writes)
        return b.op(dve, lambda e: e.tensor_scalar(out=out, in0=in0, scalar1=s1_, scalar2=s2_, op0=op0, op1=op1),
                    reads, writes)

    def cp_(out, in_, reads, writes):
        return b.op(dve, lambda e: e.tensor_copy(out=out, in_=in_), reads, writes)

    def stt_(out, in0, sc_, in1, op0, op1, reads, writes):
        return b.op(dve, lambda e: e.scalar_tensor_tensor(out=out, in0=in0, scalar=sc_, in1=in1, op0=op0, op1=op1),
                    reads, writes)

    def fl(t):
        return t[:, :, :, :].rearrange("p d c h -> p (d c h)")

    KV_d = nc.dram_tensor("KV_d", [TOK, 3072], BF16)
    KVb = main.bufs_n(12)
    KQ_d = nc.dram_tensor("KQ_d", [16, 128, NT], BF16)
    KQb = main.bufs_n(16)
    SIGO_d = nc.dram_tensor("SIGO_d", [16, 128, NT], F32)
    SIGOb = main.bufs_n(16)
    SIGM_d = nc.dram_tensor("SIGM_d", [32, 128, NT], F32)
    SIGMb = main.bufs_n(32)
    U_d = nc.dram_tensor("U_d", [8, 128, NT], F32)
    Ub = main.bufs_n(8)
    CT_d = nc.dram_tensor("CT_d", [8, 128, NT], BF16)
    CTb = main.bufs_n(8)
    CCTX_d = nc.dram_tensor("CCTX_d", [128, 4112], F32)
    CCTXb = main.buf()
    XS_d = nc.dram_tensor("XS_d", [128, XW], F32)
    XSb = main.bufs_n(6)
    XR_d = nc.dram_tensor("XR_d", [8 * 128, XW], F32, addr_space="Shared")
    XRb = main.buf()
    BAR_d = nc.dram_tensor("BAR_d", [16, 32], F32)
    BARR_d = nc.dram_tensor("BARR_d", [8 * 16, 32], F32, addr_space="Shared")
    BARb = main.buf()
    ccsem = Sem(b, "ccsem")
    ccsem2 = Sem(b, "ccsem2")

    with Scope(b) as sc:
        BIAS = sc.sb("BIASROW", [128, 3104], F32)
        BIASb = sc.buf()
        b.qsp.dma(BIAS[:, :], b_in_d[0:1, 0:3104].to_broadcast([128, 3104]), writes=[BIASb])
        KST = Ring(sc, "kst", 2, [128, 10, 256], BF16)

        def htile(kt, tt):
            if tt < 8:
                return HL[:, kt, tt * 128:(tt + 1) * 128], HLb[kt][tt // 4]
            return HC[:, kt, (tt - 8) * 128:(tt - 7) * 128], HCb[kt]

        for pj in range(13):
            c0 = pj * 256
            ncol = 256 if pj < 12 else 32
            pt, pb_ = load_panel(w_in_d, 0, 16, c0, ncol)
            if pj < 12:
                st, stb_ = KST.next()
            for tt in range(10):
                bk, bkb = b.next_bank()
                for kt in range(KT):
                    ha, hb_ = htile(kt, tt)
                    b.op(pe, lambda e: e.matmul(bk[:, 0:ncol], lhsT=ha, rhs=pt[:, kt, 0:ncol], start=(kt == 0),
                                               stop=(kt == KT - 1)),
                         reads=[pb_, hb_], writes=[bkb], inc=(kt == KT - 1))
                if pj < 12:
                    tt_(st[:, tt, :], bk[:, 0:256], BIAS[:, c0:c0 + 256], ALU.add, [bkb, BIASb], [stb_])
                else:
                    tt_(GATES[:, tt, :], bk[:, 0:32], BIAS[:, 3072:3104], ALU.add, [bkb, BIASb], [GATESb])
            if pj < 12:
                b.qsp.dma(KV_d[:, c0:c0 + 256].rearrange("(t p) c -> p t c", p=128), st[:, :, :], reads=[stb_],
                          writes=[KVb[pj]])
    s1.__exit__(None, None, None)

    G4 = GATES[:, :, :].rearrange("p c (d k h) -> p c d k h", d=2, k=2, h=8)
    LF, GB, BB, RR, RMB, WW, CL, SFX, LAM, WA = (GD[n] for n in ("LF", "GB", "BB", "RR", "RMB", "WW", "CL", "SFX",
                                                                "LAM", "WA"))
    gd = [GDb]
    for d_ in range(2):
        b.op(act, lambda e: e.activation(out=LF[:, d_, :, :], in_=G4[:, :, d_, 1, :], func=AF.Exp, scale=-1.0),
             reads=[GATESb], writes=gd)
    ts_(fl(LF), fl(LF), 1.0, ALU.add, gd, gd)
    b.op(act, lambda e: e.activation(out=fl(LF), in_=fl(LF), func=AF.Ln), reads=gd, writes=gd)
    ts_(fl(LF), fl(LF), -1.0, ALU.mult, gd, gd)
    bkA, bkAb = b.next_bank()
    b.op(pe, lambda e: e.matmul(bkA[:, 0:80], lhsT=ULE, rhs=LF[:, 0, :, :].rearrange("p c h -> p (c h)"),
                               start=True, stop=True), reads=[GDb, CONSTb], writes=[bkAb])
    b.op(pe, lambda e: e.matmul(bkA[:, 80:160], lhsT=UGE, rhs=LF[:, 1, :, :].rearrange("p c h -> p (c h)"),
                               start=True, stop=True), reads=[GDb, CONSTb], writes=[bkAb])
    bkG, bkGb = b.next_bank()
    b.op(pe, lambda e: e.matmul(bkG[:, 0:160], lhsT=ONES, rhs=fl(LF), start=True, stop=True),
         reads=[GDb, CONSTb], writes=[bkGb])
    cp_(fl(BB), bkA[:, 0:160], [bkAb], gd)
    cp_(fl(GB), bkG[:, 0:160], [bkGb], gd)
    for d_ in range(2):
        tt_(RR[:, d_, :, :], G4[:, :, d_, 0, :], BB[:, d_, :, :], ALU.subtract, [GATESb, GDb], gd)
    bkT, bkTb = b.next_bank()
    for d_ in range(2):
        b.op(pe, lambda e: e.transpose(out=bkT[0:80, d_ * 128:(d_ + 1) * 128],
                                      in_=RR[:, d_, :, :].rearrange("p c h -> p (c h)"), identity=IDENT),
             reads=[GDb, CONSTb], writes=[bkTb])
    b.op(dve, lambda e: e.tensor_reduce(out=RM2[0:80, 0:2], in_=bkT[0:80, 0:256].rearrange("p (d t) -> p d t", d=2),
                                       axis=AX.X, op=ALU.max), reads=[bkTb], writes=gd)
    for d_ in range(2):
        ts_(DG[0:80, d_, :], CONST[0:80, 0:80], RM2[0:80, d_:d_ + 1], ALU.mult, [GDb, CONSTb], gd)
    bkR, bkRb = b.next_bank()
    for d_ in range(2):
        b.op(pe, lambda e: e.matmul(bkR[:, d_ * 80:(d_ + 1) * 80], lhsT=CONST[0:80, 384:512], rhs=DG[0:80, d_, :],
                                   start=True, stop=True), reads=[GDb, CONSTb], writes=[bkRb])
    cp_(fl(RMB), bkR[:, 0:160], [bkRb], gd)
    tt_(fl(WW), fl(RR), fl(RMB), ALU.subtract, gd, gd)
    b.op(act, lambda e: e.activation(out=fl(WW), in_=fl(WW), func=AF.Exp), reads=gd, writes=gd)
    tt_(fl(CL), fl(BB), fl(RMB), ALU.add, gd, gd)
    b.op(act, lambda e: e.activation(out=fl(CL), in_=fl(CL), func=AF.Exp, scale=-1.0), reads=gd, writes=gd)
    ts_(fl(CL), fl(CL), float(np.sqrt(128.0)), ALU.mult, gd, gd)
    cp_(SFX[:, 0, 7, :], GB[:, 0, 7, :], gd, gd)
    for c in range(6, -1, -1):
        tt_(SFX[:, 0, c, :], SFX[:, 0, c + 1, :], GB[:, 0, c, :], ALU.add, gd, gd)
    cp_(SFX[:, 0, 9, :], GB[:, 0, 9, :], gd, gd)
    tt_(SFX[:, 0, 8, :], SFX[:, 0, 9, :], GB[:, 0, 8, :], ALU.add, gd, gd)
    cp_(SFX[:, 1, 0, :], GB[:, 1, 0, :], gd, gd)
    for c in range(1, 8):
        tt_(SFX[:, 1, c, :], SFX[:, 1, c - 1, :], GB[:, 1, c, :], ALU.add, gd, gd)
    cp_(SFX[:, 1, 8, :], GB[:, 1, 8, :], gd, gd)
    tt_(SFX[:, 1, 9, :], SFX[:, 1, 8, :], GB[:, 1, 9, :], ALU.add, gd, gd)
    tt_(fl(LAM), fl(SFX), fl(RMB), ALU.add, gd, gd)
    for d_ in range(2):
        b.op(dve, lambda e: e.tensor_reduce(out=MAGG[:, d_, :], in_=LAM[:, d_, 0:8, :].rearrange("p c h -> p h c"),
                                           axis=AX.X, op=ALU.max), reads=gd, writes=gd)
        tt_(MCTX[:, d_, :], LAM[:, d_, 8, :], LAM[:, d_, 9, :], ALU.max, gd, gd)
    tt_(fl(WA), fl(RR), fl(SFX), ALU.add, gd, gd)
    for d_ in range(2):
        tt_(WA[:, d_, 0:8, :], WA[:, d_, 0:8, :], MAGG[:, d_:d_ + 1, :].to_broadcast([128, 8, 8]), ALU.subtract,
            gd, gd)
        tt_(WA[:, d_, 8:10, :], WA[:, d_, 8:10, :], MCTX[:, d_:d_ + 1, :].to_broadcast([128, 2, 8]), ALU.subtract,
            gd, gd)
    b.op(act, lambda e: e.activation(out=fl(WA), in_=fl(WA), func=AF.Exp), reads=gd, writes=gd)
    cp_(SCAL[:, 0:16], MAGG[:, :, :].rearrange("p d h -> p (d h)"), gd, gd)
    cp_(SCAL[:, 16:24], SFX[:, 0, 0, :], gd, gd)
    cp_(SCAL[:, 24:32], SFX[:, 1, 7, :], gd, gd)
    b.qsp.dma(XS_d[:, XOFF_SC:XOFF_SC + 32], SCAL[:, :], reads=gd, writes=[XSb[1]])

    def load_head(KH, VX, h):
        kh, khb = KH.next()
        vx, vxb = VX.next()
        b.qsp.dma(kh[:, :, :], KV_d[:, h * 128:(h + 1) * 128].rearrange("(c p) k -> p c k", p=128),
                  reads=[KVb[h // 2]], writes=[khb])
        b.qsp.dma(vx[:, :, 0:256], KV_d[:, 1024 + h * 256:1024 + (h + 1) * 256].rearrange("(c p) k -> p c k", p=128),
                  reads=[KVb[4 + h]], writes=[vxb])
        return kh, khb, vx, vxb

    with Scope(b) as sc:
        KH = Ring(sc, "kh", 2, [128, 10, 128], BF16)
        VX = Ring(sc, "vx", 2, [128, 10, 257], BF16)
        for i in range(2):
            b.op(dve, lambda e: e.memset(VX.t[i][:, :, 256:257], 1.0), reads=[], writes=[VX.bf[i]])
        VW = Ring(sc, "vw", 4, [128, 257], BF16)
        XSND = sc.sb("xsnd", [128, 2, 8, 257], F32)
        XSNDb = sc.buf()
        XCTX = sc.sb("xctx", [128, 2, 8, 257], F32)
        XCTXb = sc.buf()
        nvw = 0
        for h in range(8):
            kh, khb, vx, vxb = load_head(KH, VX, h)
            for d_ in range(2):
                for (cl_, dst, dstb) in ((list(range(8)), XSND, XSNDb), ([8, 9], XCTX, XCTXb)):
                    bk, bkb = b.next_bank()
                    for ci, c in enumerate(cl_):
                        vw, vwb = VW.next()
                        nvw += 1
                        if nvw % 2 == 0:
                            b.op(act, lambda e: e.activation(out=vw[:, :], in_=vx[:, c, :], func=AF.Identity,
                                                            scale=WA[:, d_, c, h:h + 1]),
                                 reads=[vxb, GDb], writes=[vwb])
                        else:
                            ts_(vw[:, :], vx[:, c, :], WA[:, d_, c, h:h + 1], ALU.mult, [vxb, GDb], [vwb])
                        b.op(pe, lambda e: e.matmul(bk[:, 0:257], lhsT=kh[:, c, :], rhs=vw[:, :], start=(ci == 0),
                                                   stop=(ci == len(cl_) - 1)),
                             reads=[khb, vwb], writes=[bkb], inc=True)
                    b.op(act, lambda e: e.activation(out=dst[:, d_, h, :], in_=bk[:, 0:257], func=AF.Identity),
                         reads=[bkb], writes=[dstb])
        b.qsp.dma(XS_d[:, 0:4112], XSND[:, :, :, :].rearrange("p d h e -> p (d h e)"), reads=[XSNDb],
                  writes=[XSb[0]])
        b.qsp.dma(CCTX_d[:, :], XCTX[:, :, :, :].rearrange("p d h e -> p (d h e)"), reads=[XCTXb], writes=[CCTXb])

    def proj_fm(col_tiles, evacs):
        i = 0
        while i < len(col_tiles):
            c0 = col_tiles[i]
            two = (i + 1 < len(col_tiles) and col_tiles[i + 1] == c0 + 128)
            ncol = 256 if two else 128
            pt, pb_ = load_panel(w_in_d, 0, 16, c0, ncol)
            for ct in range(2 if two else 1):
                bks = [b.next_bank() for _ in (0, 1)]
                b.prewait(pe, bks)
                for kt in range(KT):
                    for pi, p in enumerate((P0, P1)):
                        b.op(pe, lambda e: e.matmul(bks[pi][0][:, 0:512], lhsT=pt[:, kt, ct * 128:(ct + 1) * 128],
                                                   rhs=HL[:, kt, p.s:p.s + 512], start=(kt == 0),
                                                   stop=(kt == KT - 1)),
                             reads=[pb_, HLb[kt][pi]], writes=[bks[pi][1]], inc=(kt == KT - 1 and pi == 1))
                evacs[i + ct](bks)
            i += 2 if two else 1

    def ev_bias_store(sc_ring, bname, bidx, func, dram_ap, dram_buf, extra=None):
        def f(bks):
            st, stb_ = sc_ring.next()
            for pi in range(2):
                b.op(act, lambda e: e.activation(out=st[:, pi * 512:(pi + 1) * 512], in_=bks[pi][0][:, :], func=func,
                                                bias=P(bname, bidx)), reads=[bks[pi][1], PRMb], writes=[stb_])
            b.qsp.dma(dram_ap, st[:, :], reads=[stb_], writes=[dram_buf])
        return f

    with Scope(b) as sc:
        EST = Ring(sc, "est", 3, [128, NT], F32)
        EBT = Ring(sc, "ebt", 2, [128, NT], BF16)
        AST = Ring(sc, "ast", 2, [128, NT], F32)
        SGT = Ring(sc, "sgt", 2, [128, 512], F32)
        ast = {}

        def ev_a(t):
            def f(bks):
                at, atb = AST.next()
                ast[t] = (at, atb)
                for pi in range(2):
                    b.op(act, lambda e: e.activation(out=at[:, pi * 512:(pi + 1) * 512], in_=bks[pi][0][:, :],
                                                    func=AF.Identity, bias=P("bglu", t)),
                         reads=[bks[pi][1], PRMb], writes=[atb])
            return f

        def ev_g(t):
            def f(bks):
                at, atb = ast[t]
                st, stb_ = EST.next()
                for pi in range(2):
                    sg, sgb = SGT.next()
                    b.op(act, lambda e: e.activation(out=sg[:, :], in_=bks[pi][0][:, :], func=AF.Sigmoid,
                                                    bias=P("bglu", 8 + t)), reads=[bks[pi][1], PRMb], writes=[sgb])
                    tt_(st[:, pi * 512:(pi + 1) * 512], at[:, pi * 512:(pi + 1) * 512], sg[:, :], ALU.mult,
                        [atb, sgb], [stb_])
                b.qsp.dma(U_d[t, :, :], st[:, :], reads=[stb_], writes=[Ub[t]])
                if t >= 4:
                    b.qsp.dma(XS_d[:, XOFF_U + (t - 4) * 1024:XOFF_U + (t - 3) * 1024], st[:, :], reads=[stb_],
                              writes=[XSb[t - 2]])
            return f

        cols, evs = [], []
        for t0 in range(0, 8, 2):
            cols += [OFF_GLU + t0 * 128, OFF_GLU + (t0 + 1) * 128, OFF_GLU + 1024 + t0 * 128,
                     OFF_GLU + 1024 + (t0 + 1) * 128]
            evs += [ev_a(t0), ev_a(t0 + 1), ev_g(t0), ev_g(t0 + 1)]
        proj_fm(cols, evs)
        cols = [t * 128 for t in range(8)] + [OFF_Q + t * 128 for t in range(8)]
        evs = [ev_bias_store(EBT, "bk", t, AF.Identity, KQ_d[t, :, :], KQb[t]) for t in range(8)]
        evs += [ev_bias_store(EBT, "bq", t, AF.Identity, KQ_d[8 + t, :, :], KQb[8 + t]) for t in range(8)]
        proj_fm(cols, evs)

        deps = b.deps_for(XSb, [XRb])
        b.pool.waits(deps)
        nc.gpsimd.collective_compute("AllGather", ALU.bypass, replica_groups=[list(range(NCORES))],
                                     ins=[XS_d.ap().opt()], outs=[XR_d.ap().opt()]).then_inc(ccsem.h)
        b.note(XSb, [XRb], (ccsem, 1))

        cols = [OFF_O + t * 128 for t in range(16)]
        evs = [ev_bias_store(EST, "bo", t, AF.Sigmoid, SIGO_d[t, :, :], SIGOb[t]) for t in range(16)]
        proj_fm(cols, evs)
        cols = [OFF_MERGE + t * 128 for t in range(32)]
        evs = [ev_bias_store(EST, "bmg", t, AF.Sigmoid, SIGM_d[t, :, :], SIGMb[t]) for t in range(32)]
        proj_fm(cols, evs)
        b.qpl.dma(BAR_d[:, :], flags_d[0:16, 0:32], writes=[BARb])
        deps = b.deps_for([BARb, XRb], [])
        b.pool.waits(deps)
        nc.gpsimd.collective_compute("AllGather", ALU.bypass, replica_groups=[list(range(NCORES))],
                                     ins=[BAR_d.ap().opt()], outs=[BARR_d.ap().opt()]).then_inc(ccsem2.h)
        XRb.w = (ccsem2, 1)

    mod_vec(8)

    cm = [CMb]
    b.qsp.dma(SCS[:, :, :], XR_d[:, XOFF_SC:XOFF_SC + 32].rearrange("(j p) w -> p j w", p=128), reads=[XRb],
              writes=cm)
    GF, SG, LJ = CMB["GF"], CMB["SG"], CMB["LJ"]
    for d_ in range(2):
        fa = lambda j: FL[:, 16 * d_ + j:16 * d_ + j + 1]
        fn_ = lambda j: FL[:, 16 * d_ + 8 + j:16 * d_ + 8 + j + 1]
        for j in range(8):
            ts_(GF[:, d_, j, :], SCS[:, j, 16 + 8 * d_:24 + 8 * d_], fa(j), ALU.mult, [CMb, FLb], cm)
            ts_(LJ[:, d_, j, :], SCS[:, j, 8 * d_:8 * d_ + 8], fa(j), ALU.mult, [CMb, FLb], cm, s2_=fn_(j),
                op1=ALU.add)
        order = list(range(7, -1, -1)) if d_ == 0 else list(range(8))
        b.op(dve, lambda e: e.memset(SG[:, d_, order[0], :], 0.0), reads=[], writes=cm)
        for a_, nx in zip(order[1:], order[:-1]):
            tt_(SG[:, d_, a_, :], SG[:, d_, nx, :], GF[:, d_, nx, :], ALU.add, cm, cm)
        tt_(CO[:, d_, 8, :], SG[:, d_, order[-1], :], GF[:, d_, order[-1], :], ALU.add, cm, cm)
        tt_(CO[:, d_, 8, :], CO[:, d_, 8, :], MCTX[:, d_, :], ALU.add, [CMb, GDb], cm)
        tt_(LJ[:, d_, :, :], LJ[:, d_, :, :], SG[:, d_, :, :], ALU.add, cm, cm)
        cp_(MST[:, d_, :], CO[:, d_, 8, :], cm, cm)
        for j in range(8):
            tt_(MST[:, d_, :], MST[:, d_, :], LJ[:, d_, j, :], ALU.max, cm, cm)
        tt_(CO[:, d_, 0:8, :], LJ[:, d_, :, :], MST[:, d_:d_ + 1, :].to_broadcast([128, 8, 8]), ALU.subtract, cm, cm)
        tt_(CO[:, d_, 8, :], CO[:, d_, 8, :], MST[:, d_, :], ALU.subtract, cm, cm)
    b.op(act, lambda e: e.activation(out=CO[:, :, :, :].rearrange("p d j h -> p (d j h)"),
                                    in_=CO[:, :, :, :].rearrange("p d j h -> p (d j h)"), func=AF.Exp),
         reads=cm, writes=cm)

    def yc(t):
        return HL[:, 2 * t:2 * t + 2, :].bitcast(F32).rearrange("p a b -> p (a b)")

    def ycb(t):
        return [HLb[2 * t][0], HLb[2 * t][1], HLb[2 * t + 1][0], HLb[2 * t + 1][1]]

    with Scope(b) as sc:
        UTL = Ring(sc, "utl", 2, [128, NT], F32)
        UE = sc.sb("UE", [128, 46, 64], F32)
        UEb = sc.buf()
        HJ = Ring(sc, "hj", 2, [128, NT], F32)
        for t in range(8):
            acc = yc(t)
            accb = ycb(t)
            if t < 4:
                u, ub_ = UTL.next()
                b.qsp.dma(u[:, :], U_d[t, :, :], reads=[Ub[t]], writes=[ub_])
                ts_(acc, u[:, :], P("cw", 15 * 8 + t), ALU.mult, [ub_, PRMb], accb, s2_=P("cb", t), op1=ALU.add)
                a3 = acc.rearrange("p (r c) -> p r c", c=64)
                u3 = u[:, :].rearrange("p (r c) -> p r c", c=64)
                for j in range(31):
                    o = j - 15
                    if o == 0:
                        continue
                    if o > 0:
                        stt_(a3[:, :, 0:64 - o], u3[:, :, o:64], P("cw", j * 8 + t), a3[:, :, 0:64 - o], ALU.mult,
                             ALU.add, [ub_, PRMb] + accb, accb)
                    else:
                        stt_(a3[:, :, -o:64], u3[:, :, 0:64 + o], P("cw", j * 8 + t), a3[:, :, -o:64], ALU.mult,
                             ALU.add, [ub_, PRMb] + accb, accb)
            else:
                b.qsp.dma(UE[:, 15:31, :].rearrange("p r c -> p (r c)"), U_d[t, :, :], reads=[Ub[t]], writes=[UEb])
                uef = UE[:, :, :].rearrange("p r c -> p (r c)")
                for j in range(8):
                    hj, hjb = HJ.next()
                    co = XOFF_U + (t - 4) * 1024
                    b.qsp.dma(hj[:, :], XR_d[j * 128:(j + 1) * 128, co:co + 1024], reads=[XRb], writes=[hjb])
                    fp_ = FL[:, 32 + j:33 + j]
                    fn2 = FL[:, 40 + j:41 + j]
                    if j == 0:
                        ts_(uef[:, 0:960], hj[:, 64:1024], fp_, ALU.mult, [hjb, FLb], [UEb])
                        ts_(uef[:, 31 * 64:46 * 64], hj[:, 0:960], fn2, ALU.mult, [hjb, FLb], [UEb])
                    else:
                        stt_(uef[:, 0:960], hj[:, 64:1024], fp_, uef[:, 0:960], ALU.mult, ALU.add, [hjb, FLb, UEb],
                             [UEb])
                        stt_(uef[:, 31 * 64:46 * 64], hj[:, 0:960], fn2, uef[:, 31 * 64:46 * 64], ALU.mult, ALU.add,
                             [hjb, FLb, UEb], [UEb])
                ts_(acc, uef[:, 0:1024], P("cw", 0 * 8 + t), ALU.mult, [UEb, PRMb], accb, s2_=P("cb", t),
                    op1=ALU.add)
                for j in range(1, 31):
                    stt_(acc, uef[:, j * 64:j * 64 + 1024], P("cw", j * 8 + t), acc, ALU.mult, ALU.add,
                         [UEb, PRMb] + accb, accb)
        LNT = Ring(sc, "clnt", 3, [128, 512], F32)
        MEAN = sc.sb("cln_mean", [128, 512], F32)
        RSTD = sc.sb("cln_rstd", [128, 512], F32)
        MSQ = sc.sb("cln_msq", [128, 512], F32)
        CST_ = Ring(sc, "cst", 2, [128, 512], BF16)
        stb = sc.buf()
        for pi in range(2):
            sl = slice(pi * 512, (pi + 1) * 512)
            rb_ = lambda t: [HLb[2 * t][0], HLb[2 * t][1], HLb[2 * t + 1][0], HLb[2 * t + 1][1]]
            s1_, s1b = b.next_bank()
            for t in range(8):
                b.op(pe, lambda e: e.matmul(s1_[:, :], lhsT=ONES, rhs=yc(t)[:, sl], start=(t == 0), stop=(t == 7)),
                     reads=rb_(t) + [CONSTb], writes=[s1b], inc=(t == 7))
            s2_b, s2b = b.next_bank()
            for t in range(8):
                sq, sqb = LNT.next()
                b.op(act, lambda e: e.activation(out=sq[:, :], in_=yc(t)[:, sl], func=AF.Square), reads=rb_(t),
                     writes=[sqb])
                b.op(pe, lambda e: e.matmul(s2_b[:, :], lhsT=ONES, rhs=sq[:, :], start=(t == 0), stop=(t == 7)),
                     reads=[sqb, CONSTb], writes=[s2b], inc=True)
            ts_(MEAN[:, :], s1_[:, :], 1.0 / 1024, ALU.mult, [s1b], [stb])
            tt_(MSQ[:, :], MEAN[:, :], MEAN[:, :], ALU.mult, [stb], [stb])
            stt_(MSQ[:, :], s2_b[:, :], 1.0 / 1024, MSQ[:, :], ALU.mult, ALU.subtract, [s2b, stb], [stb])
            ts_(MSQ[:, :], MSQ[:, :], EPS, ALU.add, [stb], [stb])
            b.op(act, lambda e: e.activation(out=RSTD[:, :], in_=MSQ[:, :], func=AF.Sqrt), reads=[stb], writes=[stb])
            b.op(dve, lambda e: e.reciprocal(out=RSTD[:, :], in_=RSTD[:, :]), reads=[stb], writes=[stb])
            for t in range(8):
                tq, tqb = LNT.next()
                tt_(tq[:, :], yc(t)[:, sl], MEAN[:, :], ALU.subtract, rb_(t) + [stb], [tqb])
                tt_(tq[:, :], tq[:, :], RSTD[:, :], ALU.mult, [tqb, stb], [tqb])
                cs, csb = CST_.next()
                b.op(act, lambda e: e.activation(out=cs[:, :], in_=tq[:, :], func=AF.Silu, scale=P("clg", t),
                                                bias=P("clb", t)), reads=[tqb, PRMb], writes=[csb])
                b.qsp.dma(CT_d[t, :, sl], cs[:, :], reads=[csb], writes=[CTb[t]])

    p2 = [P2b]
    for d_ in range(2):
        cp_(MS[:, d_, 0, :], MST[:, d_, :], cm, p2)
        for t in range(8):
            c = t if d_ == 0 else 7 - t
            tt_(MXS[:, d_, t, :], MS[:, d_, t, :], RMB[:, d_, c, :], ALU.max, [P2b, GDb], p2)
            tt_(MS[:, d_, t + 1, :], MXS[:, d_, t, :], GB[:, d_, c, :], ALU.add, [P2b, GDb], p2)
            tt_(EXA[:, 1, d_, t, :], RMB[:, d_, c, :], MXS[:, d_, t, :], ALU.subtract, [P2b, GDb], p2)
            tt_(EXA[:, 2, d_, t, :], MS[:, d_, t, :], RMB[:, d_, c, :], ALU.subtract, [P2b, GDb], p2)
        tt_(EXA[:, 0, d_, :, :], MS[:, d_, 0:8, :], MXS[:, d_, :, :], ALU.subtract, p2, p2)
    exf = EXA[:, :, :, :, :].rearrange("p k d t h -> p (k d t h)")
    b.op(act, lambda e: e.activation(out=exf, in_=exf, func=AF.Exp), reads=p2, writes=p2)

    with Scope(b) as sc:
        CST = sc.sb("CST", [128, 2, 8, 257], F32)
        CSTb = sc.bufs_n(2, 8)
        sc2 = Scope(b)
        sc2.__enter__()
        CJ = Ring(sc2, "cj", 2, [128, 8, 257], F32)
        for d_ in range(2):
            for js in range(9):
                cj, cjb = CJ.next()
                if js == 0:
                    b.qsp.dma(cj[:, :, :].rearrange("p h e -> p (h e)"), CCTX_d[:, d_ * 2056:(d_ + 1) * 2056],
                              reads=[CCTXb], writes=[cjb])
                    coef = lambda h: CO[:, d_, 8, h:h + 1]
                else:
                    j = js - 1
                    b.qsp.dma(cj[:, :, :].rearrange("p h e -> p (h e)"),
                              XR_d[j * 128:(j + 1) * 128, d_ * 2056:(d_ + 1) * 2056], reads=[XRb], writes=[cjb])
                    coef = lambda h: CO[:, d_, j, h:h + 1]
                for h in range(8):
                    if js == 0:
                        ts_(CST[:, d_, h, :], cj[:, h, :], coef(h), ALU.mult, [cjb, CMb], [CSTb[d_][h]])
                    else:
                        stt_(CST[:, d_, h, :], cj[:, h, :], coef(h), CST[:, d_, h, :], ALU.mult, ALU.add,
                             [cjb, CMb, CSTb[d_][h]], [CSTb[d_][h]])

        sc2.__exit__(None, None, None)
        KH = Ring(sc, "kh2", 1, [128, 10, 128], BF16)
        VX = Ring(sc, "vx2", 1, [128, 10, 257], BF16)
        b.op(dve, lambda e: e.memset(VX.t[0][:, :, 256:257], 1.0), reads=[], writes=[VX.bf[0]])
        KQT = Ring(sc, "kqt", 2, [128, NT], BF16)
        VW = Ring(sc, "vw2", 4, [128, 257], BF16)
        C0S = Ring(sc, "c0s", 4, [128, 257], BF16)
        SM = Ring(sc, "sm", 4, [128, 128], BF16)
        DEN = Ring(sc, "den", 4, [128, 4], F32)
        HO = sc.sb("HO", [128, 8, 256], F32)
        HOb = sc.bufs_n(8)
        HSQ = sc.sb("HSQ", [128, 8, 256], F32)
        HSQb = sc.buf()
        HST = sc.sb("HST", [128, 4, 8], F32)
        HSTb = sc.buf()
        SGO = Ring(sc, "sgo", 1, [128, NT], F32)
        for h in range(8):
            kh, khb, vx, vxb = load_head(KH, VX, h)
            ktT, ktb = KQT.next()
            b.qsp.dma(ktT[:, :], KQ_d[h, :, :], reads=[KQb[h]], writes=[ktb])
            qtT, qtb = KQT.next()
            b.qsp.dma(qtT[:, :], KQ_d[8 + h, :, :], reads=[KQb[8 + h]], writes=[qtb])
            for t in range(8):
                U = []
                for d_ in range(2):
                    c = t if d_ == 0 else 7 - t
                    u = {"d": d_, "c": c, "csl": slice(c * 128, (c + 1) * 128), "cstb": CSTb[d_][h],
                         "mask": ULE if d_ == 0 else UGE}
                    U.append(u)
                for u in U:
                    d_, c = u["d"], u["c"]
                    u["vw"], u["vwb"] = VW.next()
                    b.op(act, lambda e: e.activation(out=u["vw"][:, :], in_=vx[:, c, :], func=AF.Identity,
                                                    scale=WW[:, d_, c, h:h + 1]), reads=[vxb, GDb],
                         writes=[u["vwb"]])
                    u["c0"], u["c0b"] = C0S.next()
                    b.op(act, lambda e: e.activation(out=u["c0"][:, :], in_=CST[:, d_, h, :], func=AF.Identity,
                                                    scale=EXA[:, 2, d_, t, h:h + 1]), reads=[u["cstb"], P2b],
                         writes=[u["c0b"]])
                for u in U:
                    u["bS"], u["bSb"] = b.next_bank()
                    b.op(pe, lambda e: e.matmul(u["bS"][:, 0:128], lhsT=ktT[:, u["csl"]], rhs=qtT[:, u["csl"]],
                                               start=True, stop=True), reads=[ktb, qtb], writes=[u["bSb"]])
                for u in U:
                    u["sm"], u["smb"] = SM.next()
                    tt_(u["sm"][:, :], u["bS"][:, 0:128], u["mask"], ALU.mult, [u["bSb"], CONSTb], [u["smb"]])
                for u in U:
                    u["bA"], u["bAb"] = b.next_bank()
                    b.op(pe, lambda e: e.matmul(u["bA"][:, 0:257], lhsT=u["sm"][:, :], rhs=u["vw"][:, :], start=True,
                                               stop=False), reads=[u["smb"], u["vwb"]], writes=[u["bAb"]], inc=False)
                    b.op(pe, lambda e: e.matmul(u["bA"][:, 0:257], lhsT=qtT[:, u["csl"]], rhs=u["c0"][:, :],
                                               start=False, stop=True), reads=[qtb, u["c0b"]], writes=[u["bAb"]])
                    if t < 7:
                        u["bC"], u["bCb"] = b.next_bank()
                        b.op(pe, lambda e: e.matmul(u["bC"][:, 0:257], lhsT=kh[:, u["c"], :], rhs=u["vw"][:, :],
                                                   start=True, stop=True), reads=[khb, u["vwb"]], writes=[u["bCb"]])
                for u in U:
                    u["dn"], u["dnb"] = DEN.next()
                    b.op(act, lambda e: e.activation(out=u["dn"][:, 0:1], in_=u["bA"][:, 256:257], func=AF.Abs),
                         reads=[u["bAb"]], writes=[u["dnb"]])
                for u in U:
                    d_, c, dn, dnb = u["d"], u["c"], u["dn"], u["dnb"]
                    ts_(dn[:, 1:2], dn[:, 0:1], CL[:, d_, c, h:h + 1], ALU.max, [dnb, GDb], [dnb])
                    b.op(dve, lambda e: e.reciprocal(out=dn[:, 2:3], in_=dn[:, 1:2]), reads=[dnb], writes=[dnb])
                    first = (d_ == 0 and c <= 3) or (d_ == 1 and c >= 4)
                    if first:
                        ts_(HO[:, c, :], u["bA"][:, 0:256], dn[:, 2:3], ALU.mult, [u["bAb"], dnb], [HOb[c]])
                    else:
                        stt_(HO[:, c, :], u["bA"][:, 0:256], dn[:, 2:3], HO[:, c, :], ALU.mult, ALU.add,
                             [u["bAb"], dnb, HOb[c]], [HOb[c]])
                if t < 7:
                    for u in U:
                        d_, cstb = u["d"], u["cstb"]
                        ts_(CST[:, d_, h, :], CST[:, d_, h, :], EXA[:, 0, d_, t, h:h + 1], ALU.mult, [cstb, P2b],
                            [cstb])
                        stt_(CST[:, d_, h, :], u["bC"][:, 0:257], EXA[:, 1, d_, t, h:h + 1], CST[:, d_, h, :],
                             ALU.mult, ALU.add, [u["bCb"], P2b, cstb], [cstb])
            b.op(dve, lambda e: e.tensor_reduce(out=HST[:, 0, :], in_=HO[:, :, :], axis=AX.X, op=ALU.add),
                 reads=HOb, writes=[HSTb])
            tt_(HSQ[:, :, :], HO[:, :, :], HO[:, :, :], ALU.mult, HOb, [HSQb])
            b.op(dve, lambda e: e.tensor_reduce(out=HST[:, 1, :], in_=HSQ[:, :, :], axis=AX.X, op=ALU.add),
                 reads=[HSQb], writes=[HSTb])
            ts_(HST[:, 0, :], HST[:, 0, :], 1.0 / 256, ALU.mult, [HSTb], [HSTb])
            tt_(HST[:, 2, :], HST[:, 0, :], HST[:, 0, :], ALU.mult, [HSTb], [HSTb])
            stt_(HST[:, 1, :], HST[:, 1, :], 1.0 / 256, HST[:, 2, :], ALU.mult, ALU.subtract, [HSTb], [HSTb])
            ts_(HST[:, 1, :], HST[:, 1, :], EPS, ALU.add, [HSTb], [HSTb])
            b.op(act, lambda e: e.activation(out=HST[:, 2, :], in_=HST[:, 1, :], func=AF.Sqrt), reads=[HSTb],
                 writes=[HSTb])
            b.op(dve, lambda e: e.reciprocal(out=HST[:, 3, :], in_=HST[:, 2, :]), reads=[HSTb], writes=[HSTb])
            for c in range(8):
                ts_(HSQ[:, c, :], HO[:, c, :], HST[:, 0, c:c + 1], ALU.subtract, [HOb[c], HSTb], [HSQb],
                    s2_=HST[:, 3, c:c + 1], op1=ALU.mult)
            for et in range(2):
                kt = 2 * h + et
                so, sob = SGO.next()
                b.qsp.dma(so[:, :], SIGO_d[kt, :, :], reads=[SIGOb[kt]], writes=[sob])
                for cg in range(2):
                    bk, bkb = b.next_bank()
                    for cc in range(4):
                        c = cg * 4 + cc
                        b.op(pe, lambda e: e.transpose(out=bk[:, cc * 128:(cc + 1) * 128],
                                                      in_=HSQ[:, c, et * 128:(et + 1) * 128], identity=IDENT),
                             reads=[HSQb, CONSTb], writes=[bkb], inc=(cc == 3))
                    stt_(HL[:, kt, cg * 512:(cg + 1) * 512], bk[:, :], P("mhg", kt), so[:, cg * 512:(cg + 1) * 512],
                         ALU.mult, ALU.mult, [bkb, PRMb, sob], [HLb[kt][cg]])

    gate_scalars(1, 1.0)
    with Scope(b) as sc:
        CTS = sc.sb("CTS", [128, 8, NT], BF16)
        CTSb = sc.bufs_n(8)
        for t in range(8):
            b.qsp.dma(CTS[:, t, :], CT_d[t, :, :], reads=[CTb[t]], writes=[CTSb[t]])
        ZT = sc.sb("ZT", [128, KT, NT], BF16)
        ZTb = sc.bufs_n(KT, 2)
        SGM = Ring(sc, "sgm", 4, [128, 512], F32)
        MT = Ring(sc, "mt", 3, [128, 512], F32)
        for dg in range(0, KT, 2):
            pm, pmb = load_panel(wmo_d, 0, 16, dg * 128, 256)
            pc, pcb = load_panel(wco_d, 0, 8, dg * 128, 256)
            for ct in range(2):
                dt_ = dg + ct
                bm = [b.next_bank() for _ in (0, 1)]
                b.prewait(pe, bm)
                for kt in range(KT):
                    for pi in range(2):
                        b.op(pe, lambda e: e.matmul(bm[pi][0][:, :], lhsT=pm[:, kt, ct * 128:(ct + 1) * 128],
                                                   rhs=HL[:, kt, pi * 512:(pi + 1) * 512], start=(kt == 0),
                                                   stop=(kt == KT - 1)),
                             reads=[pmb, HLb[kt][pi]], writes=[bm[pi][1]], inc=(kt == KT - 1 and pi == 1))
                bc = [b.next_bank() for _ in (0, 1)]
                b.prewait(pe, bc)
                for kt in range(8):
                    for pi in range(2):
                        b.op(pe, lambda e: e.matmul(bc[pi][0][:, :], lhsT=pc[:, kt, ct * 128:(ct + 1) * 128],
                                                   rhs=CTS[:, kt, pi * 512:(pi + 1) * 512], start=(kt == 0),
                                                   stop=(kt == 7)),
                             reads=[pcb, CTSb[kt]], writes=[bc[pi][1]], inc=(kt == 7 and pi == 1))
                for pi in range(2):
                    sl = slice(pi * 512, (pi + 1) * 512)
                    gm, gmb = SGM.next()
                    b.qsp.dma(gm[:, :], SIGM_d[dt_, :, sl], reads=[SIGMb[dt_]], writes=[gmb])
                    gc, gcb = SGM.next()
                    b.qsp.dma(gc[:, :], SIGM_d[16 + dt_, :, sl], reads=[SIGMb[16 + dt_]], writes=[gcb])
                    m1, m1b = MT.next()
                    tt_(m1[:, :], bm[pi][0][:, :], gm[:, :], ALU.mult, [bm[pi][1], gmb], [m1b])
                    m2, m2b = MT.next()
                    tt_(m2[:, :], bc[pi][0][:, :], gc[:, :], ALU.mult, [bc[pi][1], gcb], [m2b])
                    tt_(ZT[:, dt_, sl], m1[:, :], m2[:, :], ALU.add, [m1b, m2b], [ZTb[dt_][pi]])
        for dg in range(0, KT, 2):
            pw, pwb = load_panel(wo_d, 0, 16, dg * 128, 256)
            for ct in range(2):
                dt_ = dg + ct
                bks = [b.next_bank() for _ in (0, 1)]
                b.prewait(pe, bks)
                for kt in range(KT):
                    for pi in range(2):
                        b.op(pe, lambda e: e.matmul(bks[pi][0][:, :], lhsT=pw[:, kt, ct * 128:(ct + 1) * 128],
                                                   rhs=ZT[:, kt, pi * 512:(pi + 1) * 512], start=(kt == 0),
                                                   stop=(kt == KT - 1)),
                             reads=[pwb, ZTb[kt][pi]], writes=[bks[pi][1]], inc=(kt == KT - 1 and pi == 1))
                for pi in range(2):
                    sl = slice(pi * 512, (pi + 1) * 512)
                    stt_(XL[:, dt_, sl], bks[pi][0][:, :], DER[:, 5, dt_, 0:1], XL[:, dt_, sl], ALU.mult, ALU.add,
                         [bks[pi][1], DERb, XLb[dt_][pi]], [XLb[dt_][pi]])
    piecesL = [P0, P1]
    derive(1, 2, final=(stop_after == 1))
    with Scope(b) as sc:
        layer_norm_x(sc, piecesL, None, None, None, None, do_h=True)

    if stop_after == 1:
        write_out()
        main.__exit__(None, None, None)
        return b

    gate_scalars(2, 0.5)
    with Scope(b) as sc:
        ffn(sc, f2w13_d, f2w2_d, piecesL, None, None, None, None)
    derive(2, None, final=True)
    with Scope(b) as sc:
        layer_norm_x(sc, piecesL, None, None, None, None, do_h=False)
    write_out()
    main.__exit__(None, None, None)
    return b


def _consts():
    c = np.zeros((128, 512), np.float32)
    i = np.arange(128)
    c[:, 0:128] = np.eye(128, dtype=np.float32)
    c[:, 128:256] = (i[:, None] <= i[None, :]).astype(np.float32)
    c[:, 256:384] = (i[:, None] >= i[None, :]).astype(np.float32)
    c[:, 384:512] = 1.0
    return c


def _flags(r):
    beta, rho = divmod(r, 4)
    f = np.zeros((NFLAG,), np.float32)
    for j in range(8):
        bj, rj = divmod(j, 4)
        ff = 1.0 if (bj == beta and rj < rho) else 0.0
        fb = 1.0 if (bj == beta and rj > rho) else 0.0
        f[j] = ff
        f[8 + j] = (ff - 1.0) * 30000.0
        f[16 + j] = fb
        f[24 + j] = (fb - 1.0) * 30000.0
        f[32 + j] = 1.0 if (bj == beta and rj == rho - 1) else 0.0
        f[40 + j] = 1.0 if (bj == beta and rj == rho + 1) else 0.0
    return np.ascontiguousarray(np.broadcast_to(f[None, :], (128, NFLAG)))


_CACHE = {}


def _run(inputs, stop_after=2):
    if stop_after not in _CACHE:
        _CACHE[stop_after] = build(stop_after)
    b = _CACHE[stop_after]
    f32 = lambda a: np.ascontiguousarray(np.asarray(a, dtype=np.float32))
    shared = {}
    for k in ("w_ada", "w_in", "conv_w", "w_m_out", "w_c_out", "w_o", "ffn1_w13", "ffn1_w2", "ffn2_w13",
              "ffn2_w2", "post_ln_g", "post_ln_b"):
        shared[k] = f32(inputs[k][0])
    for k in ("b_ada", "b_in", "mh_ln_g", "conv_b", "conv_ln_g", "conv_ln_b"):
        shared[k] = f32(inputs[k][0]).reshape(1, -1)
    shared["consts"] = _consts()
    x = f32(inputs["x"])
    ctx = f32(inputs["ctx"])
    c = f32(inputs["c"])
    c_ctx = f32(inputs["c_ctx"])
    in_maps = []
    for r in range(NCORES):
        beta, rho = divmod(r, 4)
        m = {"x": np.ascontiguousarray(x[beta, rho * NT:(rho + 1) * NT]),
             "ctx": ctx[beta],
             "cvec": np.ascontiguousarray(np.stack([c[beta], c_ctx], 0)),
             "flags": _flags(r)}
        m.update(shared)
        in_maps.append({k: m[k] for k in b.in_names})
    res = run_bass_kernel_spmd(b.nc, in_maps, core_ids=list(range(NCORES)))
    return res


def kernel(**inputs):
    res = _run(inputs, 2)
    out = np.empty((2, 4096, D), np.float32)
    for r in range(NCORES):
        beta, rho = divmod(r, 4)
        out[beta, rho * NT:(rho + 1) * NT] = res.results[r]["y"]
    return out
```
